# Optimizing a Trainium2 kernel written in Bass

```python
import math
import jax
import jax.numpy as jnp
from jax import lax
import numpy as np

D_MODEL = 1024
BATCH = 4
SEQ = 4096
DEPTH = 2

GRID_W = 64
CTX_LEN = 256
EPS = 1e-6
NEG_INF = -1e30
D_A = 512
CONV_A = 31
H_B = 8
DH_B = 64
D_B = H_B * DH_B
NA_ROWS = 8
NA_COLS = 16
H_C = 4
DK_C = 128
DV_C = 128
D_CQK = H_C * DK_C
D_CV = H_C * DV_C
SHORT_CONV = 4
SC_PAD_L = SHORT_CONV // 2
SC_PAD_R = SHORT_CONV - 1 - SC_PAD_L
CHUNK = 64
ROPE_BASE = 10000.0
N_BRANCH = 3
D_FF = 2816
FFN_CONV = 3
N_IN = 2 * D_A + 3 * D_B + 2 * D_CQK + 2 * D_CV + 4 * H_C + N_BRANCH * D_MODEL

kernel_name = "hybrid_conv_na_gdn_flow_block"


def _rms_norm(x, g):
    xf = x.astype(jnp.float32)
    y = xf * lax.rsqrt(jnp.mean(xf * xf, axis=-1, keepdims=True) + EPS)
    return (y * g.astype(jnp.float32)).astype(x.dtype)


def _layer_norm(x, g, b):
    xf = x.astype(jnp.float32)
    mu = jnp.mean(xf, axis=-1, keepdims=True)
    var = jnp.mean(jnp.square(xf - mu), axis=-1, keepdims=True)
    y = (xf - mu) * lax.rsqrt(var + EPS) * g.astype(jnp.float32) + b.astype(jnp.float32)
    return y.astype(x.dtype)


def _l2norm(x):
    return x * lax.rsqrt(jnp.sum(x * x, axis=-1, keepdims=True) + EPS)


def _modulate(h, shift, scale):
    return h * (1 + scale[:, None, :]) + shift[:, None, :]


def _dwconv(x, w, pad_l, pad_r):
    return lax.conv_general_dilated(
        x, w[:, None, :].astype(x.dtype), window_strides=(1,), padding=[(pad_l, pad_r)],
        dimension_numbers=("NWC", "WIO", "NWC"), feature_group_count=x.shape[-1])


def _split_in(p):
    sizes = [2 * D_A, 3 * D_B, 2 * D_CQK + D_CV, D_CV, 2 * H_C, 2 * H_C]
    return jnp.split(p, np.cumsum(sizes).tolist(), axis=-1)


def _axial_rope_tables(n_tok, head_dim):
    t = jnp.arange(n_tok)
    row = (t // GRID_W).astype(jnp.float32)
    col = (t % GRID_W).astype(jnp.float32)
    n_freq = head_dim // 4
    inv = jnp.power(ROPE_BASE, -jnp.arange(n_freq, dtype=jnp.float32) / n_freq)
    ang_r = row[:, None, None] * inv
    ang_c = col[:, None, None] * inv
    return (jnp.cos(ang_r), jnp.sin(ang_r), jnp.cos(ang_c), jnp.sin(ang_c))


def _rotate(x, cos, sin):
    x1, x2 = jnp.split(x, 2, axis=-1)
    return jnp.concatenate([x1 * cos - x2 * sin, x1 * sin + x2 * cos], axis=-1)


def _axial_rope(x, rope):
    cr, sr, cc, sc = rope
    xr, xc = jnp.split(x, 2, axis=-1)
    return jnp.concatenate([_rotate(xr, cr, sr), _rotate(xc, cc, sc)], axis=-1)


def _conformer_conv(u, conv_w, conv_b, ln_g, ln_b):
    a, gate = jnp.split(u, 2, axis=-1)
    y = a * jax.nn.sigmoid(gate)
    y = _dwconv(y, conv_w, CONV_A // 2, CONV_A // 2) + conv_b
    y = _layer_norm(y, ln_g, ln_b)
    return jax.nn.silu(y)


def _na_qkv(p, qn_g, kn_g):
    b, n, _ = p.shape
    q, k, v = jnp.split(p, 3, axis=-1)
    q = _rms_norm(q.reshape(b, n, H_B, DH_B), qn_g)
    k = _rms_norm(k.reshape(b, n, H_B, DH_B), kn_g)
    return q, k, v.reshape(b, n, H_B, DH_B)


def _neighbourhood_attention(q, k, v, k_ctx, v_ctx, rpb, rows):
    b = q.shape[0]
    kr = min(NA_ROWS, rows)
    qg = q.reshape(b, rows, GRID_W, H_B, DH_B)
    kg = k.reshape(b, rows, GRID_W, H_B, DH_B)
    vg = v.reshape(b, rows, GRID_W, H_B, DH_B)
    r = np.arange(rows)
    row_idx = np.clip(r - kr // 2, 0, rows - kr)[:, None] + np.arange(kr)
    cidx = np.arange(GRID_W)
    cs = np.clip(cidx - NA_COLS // 2, 0, GRID_W - NA_COLS)
    col_ok = (cidx[None, :] >= cs[:, None]) & (cidx[None, :] < cs[:, None] + NA_COLS)
    dr = row_idx - r[:, None] + NA_ROWS - 1
    dc = np.clip(cidx[None, :] - cidx[:, None] + NA_COLS - 1, 0, 2 * NA_COLS - 2)
    bias = rpb[:, dr[:, None, :, None], dc[None, :, None, :]]
    kb = kg[:, row_idx]
    vb = vg[:, row_idx]
    scale = DH_B ** -0.5
    s_loc = jnp.einsum("brqhd,brkwhd->bhrqkw", qg, kb, preferred_element_type=jnp.float32) * scale + bias[None]
    s_loc = jnp.where(col_ok[:, None, :], s_loc, NEG_INF)
    s_ctx = jnp.einsum("brqhd,bchd->bhrqc", qg, k_ctx, preferred_element_type=jnp.float32) * scale
    n_loc = kr * GRID_W
    s = jnp.concatenate([s_loc.reshape(b, H_B, rows, GRID_W, n_loc), s_ctx], axis=-1)
    p = jax.nn.softmax(s, axis=-1).astype(v.dtype)
    p_loc = p[..., :n_loc].reshape(b, H_B, rows, GRID_W, kr, GRID_W)
    o = jnp.einsum("bhrqkw,brkwhd->brqhd", p_loc, vb) + jnp.einsum("bhrqc,bchd->brqhd", p[..., n_loc:], v_ctx)
    return o.reshape(b, rows * GRID_W, D_B)


def _context_attention(q, k, v):
    s = jnp.einsum("bqhd,bkhd->bhqk", q, k, preferred_element_type=jnp.float32) * DH_B ** -0.5
    p = jax.nn.softmax(s, axis=-1).astype(v.dtype)
    o = jnp.einsum("bhqk,bkhd->bqhd", p, v)
    return o.reshape(q.shape[0], q.shape[1], D_B)


def _chunk_gated_delta(q, k, v, g, beta, s0):
    b, h, t, dk = k.shape
    dv = v.shape[-1]
    n = t // CHUNK
    q = q.reshape(b, h, n, CHUNK, dk)
    k = k.reshape(b, h, n, CHUNK, dk)
    v = v.reshape(b, h, n, CHUNK, dv)
    g = jnp.cumsum(g.reshape(b, h, n, CHUNK), axis=-1)
    beta = beta.reshape(b, h, n, CHUNK)[..., None]
    incl = jnp.tril(jnp.ones((CHUNK, CHUNK), dtype=bool))
    strict = jnp.tril(jnp.ones((CHUNK, CHUNK), dtype=bool), -1)
    diff = g[..., :, None] - g[..., None, :]
    decay = jnp.where(incl, jnp.exp(jnp.where(incl, diff, 0.0)), 0.0)
    kb = k * beta
    lmat = jnp.where(strict, jnp.einsum("bhnid,bhnjd->bhnij", kb, k) * decay, 0.0)
    rhs = jnp.concatenate([v * beta, kb * jnp.exp(g)[..., None]], axis=-1)
    sol = lax.linalg.triangular_solve(lmat, rhs, left_side=True, lower=True, unit_diagonal=True)
    u, w = sol[..., :dv], sol[..., dv:]
    a_intra = jnp.where(incl, jnp.einsum("bhnid,bhnjd->bhnij", q, k) * decay, 0.0)
    q_dec = q * jnp.exp(g)[..., None]
    k_dec = k * jnp.exp(g[..., -1:] - g)[..., None]
    g_last = jnp.exp(g[..., -1])

    def step(s, xs):
        qd, kd, ui, wi, ai, gl = xs
        v_new = ui - jnp.einsum("bhcd,bhde->bhce", wi, s)
        o = jnp.einsum("bhcd,bhde->bhce", qd, s) + jnp.einsum("bhij,bhje->bhie", ai, v_new)
        s = s * gl[..., None, None] + jnp.einsum("bhcd,bhce->bhde", kd, v_new)
        return s, o

    xs = tuple(jnp.moveaxis(a, 2, 0) for a in (q_dec, k_dec, u, w, a_intra, g_last))
    s, o = lax.scan(step, s0, xs)
    return jnp.moveaxis(o, 0, 2).reshape(b, h, t, dv), s


def _delta_scan(q, k, v, g, beta, s0, reverse):
    if reverse:
        q, k, v, g, beta = (jnp.flip(a, axis=2) for a in (q, k, v, g, beta))
    o, s = _chunk_gated_delta(q, k, v, g, beta, s0)
    if reverse:
        o = jnp.flip(o, axis=2)
    return o, s


def _gdn_prepare(qkv, dec, bet, conv_w, a_log, dt_bias, rope):
    b, t, _ = qkv.shape
    y = jax.nn.silu(_dwconv(qkv, conv_w, SC_PAD_L, SC_PAD_R)).astype(jnp.float32)
    q, k, v = jnp.split(y, [D_CQK, 2 * D_CQK], axis=-1)
    q = _l2norm(q.reshape(b, t, H_C, DK_C))
    k = _l2norm(k.reshape(b, t, H_C, DK_C))
    if rope is not None:
        q = _axial_rope(q, rope)
        k = _axial_rope(k, rope)
    q = q * DK_C ** -0.5
    v = v.reshape(b, t, H_C, DV_C)
    g = -jnp.exp(a_log.astype(jnp.float32)) * jax.nn.softplus(
        dec.astype(jnp.float32).reshape(b, t, 2, H_C) + dt_bias.astype(jnp.float32))
    beta = jax.nn.sigmoid(bet.astype(jnp.float32).reshape(b, t, 2, H_C))
    g = jnp.transpose(g, (2, 0, 3, 1))
    beta = jnp.transpose(beta, (2, 0, 3, 1))
    return (jnp.swapaxes(q, 1, 2), jnp.swapaxes(k, 1, 2), jnp.swapaxes(v, 1, 2), g, beta)


def _gdn_out(o, gate, gain):
    b, t, _ = gate.shape
    o = jnp.swapaxes(o, 1, 2)
    o = o * lax.rsqrt(jnp.mean(o * o, axis=-1, keepdims=True) + EPS) * gain.astype(jnp.float32)
    y = o * jax.nn.silu(gate.astype(jnp.float32).reshape(b, t, H_C, DV_C))
    return y.reshape(b, t, D_CV).astype(gate.dtype)


def _gated_deltanet(ctx_in, lat_in, gate_ctx, gate_lat, conv_w, a_log, dt_bias, gain, rope, ctx_out):
    qc, kc, vc, gc, bc = _gdn_prepare(*ctx_in, conv_w, a_log, dt_bias, None)
    ql, kl, vl, gl, bl = _gdn_prepare(*lat_in, conv_w, a_log, dt_bias, rope)
    s0 = jnp.zeros(kc.shape[:2] + (DK_C, DV_C), jnp.float32)
    o_lat = jnp.zeros_like(vl)
    o_ctx = jnp.zeros_like(vc)
    for d in range(2):
        rev = d == 1
        oc, s_ctx = _delta_scan(qc, kc, vc, gc[d], bc[d], s0, rev)
        ol, _ = _delta_scan(ql, kl, vl, gl[d], bl[d], s_ctx, rev)
        o_lat = o_lat + ol
        o_ctx = o_ctx + oc
    y_lat = _gdn_out(o_lat, gate_lat, gain)
    y_ctx = _gdn_out(o_ctx, gate_ctx, gain) if ctx_out else None
    return y_lat, y_ctx


def _merge(gate_logits, ya, yb, yc, w_br, w_o):
    ga, gb, gc = jnp.split(jax.nn.sigmoid(gate_logits), N_BRANCH, axis=-1)
    z = ga * (ya @ w_br[0]) + gb * (yb @ w_br[1]) + gc * (yc @ w_br[2])
    return z @ w_o


def _conv_ffn(h, w_up, conv_w, conv_b, w_down):
    u = _dwconv(h @ w_up, conv_w, FFN_CONV // 2, FFN_CONV // 2) + conv_b
    gate, val = jnp.split(u, 2, axis=-1)
    return (jax.nn.silu(gate) * val) @ w_down


def setup_inputs(seed: int = 0) -> dict:
    key = jax.random.key(seed)
    ks = jax.random.split(key, 32)
    f32 = jnp.float32

    def nrm(k, shape, scale):
        return jax.random.normal(k, shape, f32) * scale

    def gain(k, shape):
        return 1.0 + 0.02 * jax.random.normal(k, shape, f32)

    a_init = jax.random.uniform(ks[17], (DEPTH, 2, H_C), f32, 1.0, 16.0)
    dt = jnp.exp(jax.random.uniform(ks[18], (DEPTH, 2, H_C), f32, math.log(1e-3), math.log(1e-1)))
    return {
        "x": nrm(ks[0], (BATCH, SEQ, D_MODEL), 1.0),
        "c": nrm(ks[1], (BATCH, D_MODEL), 1.0),
        "ctx": nrm(ks[2], (BATCH, CTX_LEN, D_MODEL), 1.0),
        "c_ctx": nrm(ks[3], (D_MODEL,), 1.0),
        "ada_w": nrm(ks[4], (DEPTH, D_MODEL, 6 * D_MODEL), 0.5 * D_MODEL ** -0.5),
        "ada_b": nrm(ks[5], (DEPTH, 6 * D_MODEL), 0.01),
        "norm1_g": gain(ks[6], (DEPTH, D_MODEL)),
        "norm2_g": gain(ks[7], (DEPTH, D_MODEL)),
        "w_in": nrm(ks[8], (DEPTH, D_MODEL, N_IN), D_MODEL ** -0.5),
        "conv_a_w": nrm(ks[9], (DEPTH, CONV_A, D_A), CONV_A ** -0.5),
        "conv_a_b": nrm(ks[10], (DEPTH, D_A), 0.01),
        "ln_a_g": gain(ks[11], (DEPTH, D_A)),
        "ln_a_b": nrm(ks[12], (DEPTH, D_A), 0.01),
        "qn_g": gain(ks[13], (DEPTH, DH_B)),
        "kn_g": gain(ks[14], (DEPTH, DH_B)),
        "rpb": nrm(ks[15], (DEPTH, H_B, 2 * NA_ROWS - 1, 2 * NA_COLS - 1), 0.1),
        "conv_c_w": nrm(ks[16], (DEPTH, SHORT_CONV, 2 * D_CQK + D_CV), SHORT_CONV ** -0.5),
        "a_log": jnp.log(a_init),
        "dt_bias": dt + jnp.log(-jnp.expm1(-dt)),
        "onorm_g": gain(ks[19], (DEPTH, DV_C)),
        "w_branch": nrm(ks[20], (DEPTH, N_BRANCH, D_A, D_MODEL), D_A ** -0.5),
        "w_out": nrm(ks[21], (DEPTH, D_MODEL, D_MODEL), D_MODEL ** -0.5),
        "ffn_up": nrm(ks[22], (DEPTH, D_MODEL, 2 * D_FF), D_MODEL ** -0.5),
        "ffn_conv_w": nrm(ks[23], (DEPTH, FFN_CONV, 2 * D_FF), FFN_CONV ** -0.5),
        "ffn_conv_b": nrm(ks[24], (DEPTH, 2 * D_FF), 0.01),
        "ffn_down": nrm(ks[25], (DEPTH, D_FF, D_MODEL), D_FF ** -0.5),
    }


def reference(x, c, ctx, c_ctx, ada_w, ada_b, norm1_g, norm2_g, w_in, conv_a_w, conv_a_b, ln_a_g, ln_a_b,
              qn_g, kn_g, rpb, conv_c_w, a_log, dt_bias, onorm_g, w_branch, w_out, ffn_up, ffn_conv_w,
              ffn_conv_b, ffn_down):
    n_lat = x.shape[1]
    n_ctx = ctx.shape[1]
    rows = n_lat // GRID_W
    rope = _axial_rope_tables(n_lat, DK_C)
    silu_c = jax.nn.silu(c)
    silu_cc = jax.nn.silu(c_ctx)[None, :]
    x_lat, x_ctx = x, ctx
    for l in range(DEPTH):
        ctx_out = l < DEPTH - 1
        m_lat = jnp.split(silu_c @ ada_w[l] + ada_b[l], 6, axis=-1)
        m_ctx = jnp.split(silu_cc @ ada_w[l] + ada_b[l], 6, axis=-1)
        h_lat = _modulate(_rms_norm(x_lat, norm1_g[l]), m_lat[0], m_lat[1])
        h_ctx = _modulate(_rms_norm(x_ctx, norm1_g[l]), m_ctx[0], m_ctx[1])
        proj = jnp.concatenate([h_ctx, h_lat], axis=1) @ w_in[l]
        a_c, na_c, qkv_c, og_c, dec_c, bet_c, mg_c = _split_in(proj[:, :n_ctx])
        a_l, na_l, qkv_l, og_l, dec_l, bet_l, mg_l = _split_in(proj[:, n_ctx:])
        ya_l = _conformer_conv(a_l, conv_a_w[l], conv_a_b[l], ln_a_g[l], ln_a_b[l])
        qb_l, kb_l, vb_l = _na_qkv(na_l, qn_g[l], kn_g[l])
        qb_c, kb_c, vb_c = _na_qkv(na_c, qn_g[l], kn_g[l])
        yb_l = _neighbourhood_attention(qb_l, kb_l, vb_l, kb_c, vb_c, rpb[l], rows)
        yc_l, yc_c = _gated_deltanet((qkv_c, dec_c, bet_c), (qkv_l, dec_l, bet_l), og_c, og_l, conv_c_w[l],
                                     a_log[l], dt_bias[l], onorm_g[l], rope, ctx_out)
        x_lat = x_lat + m_lat[2][:, None, :] * _merge(mg_l, ya_l, yb_l, yc_l, w_branch[l], w_out[l])
        h2_lat = _modulate(_rms_norm(x_lat, norm2_g[l]), m_lat[3], m_lat[4])
        x_lat = x_lat + m_lat[5][:, None, :] * _conv_ffn(h2_lat, ffn_up[l], ffn_conv_w[l], ffn_conv_b[l], ffn_down[l])
        if ctx_out:
            ya_c = _conformer_conv(a_c, conv_a_w[l], conv_a_b[l], ln_a_g[l], ln_a_b[l])
            yb_c = _context_attention(qb_c, kb_c, vb_c)
            x_ctx = x_ctx + m_ctx[2][:, None, :] * _merge(mg_c, ya_c, yb_c, yc_c, w_branch[l], w_out[l])
            h2_ctx = _modulate(_rms_norm(x_ctx, norm2_g[l]), m_ctx[3], m_ctx[4])
            x_ctx = x_ctx + m_ctx[5][:, None, :] * _conv_ffn(h2_ctx, ffn_up[l], ffn_conv_w[l], ffn_conv_b[l], ffn_down[l])
    return x_lat
```

```python
import numpy as np
from contextlib import ExitStack
from functools import partial
import concourse.bass as bass
import concourse.mybir as mybir
from concourse.bass_utils import run_bass_kernel_spmd

F32 = mybir.dt.float32
BF16 = mybir.dt.bfloat16
AF = mybir.ActivationFunctionType
ALU = mybir.AluOpType
AX = mybir.AxisListType

D = 1024
DEPTH = 2
NLAT = 4096
NCTX = 256
TALL = NLAT + NCTX
GRID_W = 64
NIN = 7696
DFF = 2816
EPS = 1e-6
NEG = -30000.0
KD = 8
SAME_SYNC = True


class Tk:
    __slots__ = ("w", "r")

    def __init__(self):
        self.w = None
        self.r = {}


class Buf:
    def __init__(self, t, name):
        self.t = t
        self.name = name
        self.tk = Tk()

    def __getitem__(self, k):
        return self.t[k]


class DT:
    def __init__(self, ap, tax, ntok):
        self.ap = ap
        self.tax = tax
        self.tks = [Tk() for _ in range((ntok + 127) // 128)]

    def tk(self, t0, n):
        a = max(t0, 0) // 128
        b = (min(t0 + n, len(self.tks) * 128) - 1) // 128
        return self.tks[a:b + 1]


class Prog:
    CE = ("pe", "dve", "act", "pool")

    def __init__(self, nc, es):
        self.nc = nc
        self.es = es
        self.q = {e: [] for e in ("pe", "dve", "act", "pool", "sp")}
        self.sems = {}
        self.cnt = {}
        for e in self.CE:
            self.sems[e] = es.enter_context(nc.semaphore("s_" + e))
            self.cnt[e] = 0
        self.dval = {}
        self.drr = {"sp": 0, "pool": 0}
        for qn in ("sp", "pool"):
            for i in range(KD):
                k = f"d_{qn}{i}"
                self.sems[k] = es.enter_context(nc.semaphore(k))
                self.dval[k] = 0
        self.seen = {e: {} for e in self.q}
        self.bufs = {}
        self.psb = []
        self.psi = 0
        self.psn = 8
        self.nins = 0

    def buf(self, name, shape, dtype):
        t = self.es.enter_context(self.nc.sbuf_tensor("sb_" + name, list(shape), dtype))
        b = Buf(t, "sb_" + name)
        self.bufs["sb_" + name] = b
        return b

    def psum_init(self):
        for i in range(8):
            t = self.es.enter_context(self.nc.psum_tensor(f"ps{i}", [128, 512], F32))
            b = Buf(t, f"ps{i}")
            self.bufs[b.name] = b
            self.psb.append(b)

    def ps(self):
        b = self.psb[self.psi]
        self.psi = (self.psi + 1) % self.psn
        return b

    def tks(self, aps):
        out = []
        for a in aps:
            if a is None or isinstance(a, (int, float)):
                continue
            if isinstance(a, Tk):
                out.append(a)
            elif isinstance(a, Buf):
                out.append(a.tk)
            else:
                nm = a.tensor.name
                if nm == "sb_arena":
                    cb = (a.offset % a.ap[0][0]) * (2 if a.dtype == BF16 else 4)
                    for (lo, hi, tk) in self.views:
                        if lo <= cb < hi:
                            out.append(tk)
                            break
                    else:
                        raise RuntimeError("arena view not found")
                else:
                    b = self.bufs.get(nm)
                    if b is not None:
                        out.append(b.tk)
        return out

    def arena_init(self, nbytes):
        self.arena = self.es.enter_context(self.nc.sbuf_tensor("sb_arena", [128, nbytes // 4], F32))
        self.arena_bytes = nbytes
        self.views = []
        self.aoff = 0

    def arena_reset(self):
        self.barrier()
        self.views = []
        self.aoff = 0

    def av(self, ncols, dtype=F32):
        nb = ncols * (2 if dtype == BF16 else 4)
        nb = (nb + 63) // 64 * 64
        lo = self.aoff
        assert lo + nb <= self.arena_bytes, ("arena overflow", lo, nb)
        self.aoff += nb
        ap = self.arena[:, lo // 4:(lo + nb) // 4]
        if dtype == BF16:
            ap = ap.bitcast(BF16)
        ap = ap[:, 0:ncols]
        b = Buf(ap, "view")
        self.views.append((lo, lo + nb, b.tk))
        return b

    def barrier(self):
        allw = [(k, self.cnt[k]) for k in self.CE if self.cnt[k] > 0] + [(k, v) for k, v in self.dval.items() if v > 0]
        for e in self.q:
            waits = []
            for k, v in allw:
                if self.seen[e].get(k, 0) >= v:
                    continue
                self.seen[e][k] = v
                waits.append((self.sems[k], v))
            if waits:
                self.q[e].append((waits, None, None))

    def _deps(self, r, w):
        deps = {}
        for t in r:
            if t.w is not None and t.w[1] > deps.get(t.w[0], 0):
                deps[t.w[0]] = t.w[1]
        for t in w:
            if t.w is not None and t.w[1] > deps.get(t.w[0], 0):
                deps[t.w[0]] = t.w[1]
            for k, v in t.r.items():
                if v > deps.get(k, 0):
                    deps[k] = v
        return deps

    def op(self, eng, fn, r=(), w=(), signal=True):
        r = self.tks(r)
        w = self.tks(w)
        deps = self._deps(r, w)
        seen = self.seen[eng]
        waits = []
        for k, v in deps.items():
            if (k == eng and (not SAME_SYNC or v > self.cnt[eng])) or seen.get(k, 0) >= v:
                continue
            seen[k] = v
            waits.append((self.sems[k], v))
        if signal:
            self.cnt[eng] += 1
            val = self.cnt[eng]
        else:
            val = self.cnt[eng] + 1
        self.q[eng].append((waits, fn, (self.sems[eng], 1) if signal else None))
        self.nins += 1
        for t in r:
            if t.r.get(eng, 0) < val:
                t.r[eng] = val
        for t in w:
            t.w = (eng, val)
            t.r = {}

    def dma(self, qn, out, in_, r=(), w=(), **kw):
        r = self.tks(list(r) + [in_])
        w = self.tks(list(w) + [out])
        i = self.drr[qn]
        self.drr[qn] = (i + 1) % KD
        key = f"d_{qn}{i}"
        deps = self._deps(r, w)
        prev = self.dval[key]
        if prev > deps.get(key, 0):
            deps[key] = prev
        seen = self.seen[qn]
        waits = []
        for k, v in deps.items():
            if v <= 0 or seen.get(k, 0) >= v:
                continue
            seen[k] = v
            waits.append((self.sems[k], v))
        newv = prev + 16
        self.dval[key] = newv
        self.q[qn].append((waits, lambda e: e.dma_start(out=out, in_=in_, **kw), (self.sems[key], 16)))
        self.nins += 1
        for t in r:
            t.r[key] = newv
        for t in w:
            t.w = (key, newv)
            t.r = {}

    def finish(self):
        for qn in ("sp", "pool"):
            waits = []
            for i in range(KD):
                k = f"d_{qn}{i}"
                if self.dval[k] > 0:
                    waits.append((self.sems[k], self.dval[k]))
            self.q[qn].append((waits, None, None))

    def emit(self):
        nc = self.nc

        def mk(name):
            lst = self.q[name]

            def body(e):
                for waits, fn, inc in lst:
                    for s, v in waits:
                        e.wait_ge(s, v)
                    if fn is None:
                        continue
                    ins = fn(e)
                    if inc is not None:
                        ins.then_inc(inc[0], inc[1])
            return body

        with nc.Block() as block:
            block.tensor(mk("pe"))
            block.vector(mk("dve"))
            block.scalar(mk("act"))
            block.gpsimd(mk("pool"))
            block.sync(mk("sp"))

    def mm(self, out, lhsT, rhs, start=True, stop=True, signal=None, r=(), w=()):
        if signal is None:
            signal = stop
        self.op("pe", lambda e: e.matmul(out, lhsT=lhsT, rhs=rhs, start=start, stop=stop),
                r=[lhsT, rhs] + list(r), w=[out] + list(w), signal=signal)

    def tr(self, out, in_, ident, signal=True):
        self.op("pe", lambda e: e.transpose(out, in_, ident), r=[in_, ident], w=[out], signal=signal)

    def act(self, out, in_, func, bias=None, scale=None, accum_out=None, eng="act"):
        kw = {}
        if bias is not None:
            kw["bias"] = bias
        if scale is not None:
            kw["scale"] = scale
        if accum_out is not None:
            kw["accum_out"] = accum_out
        self.op(eng, lambda e: e.activation(out, in_, func, **kw), r=[in_, bias, scale], w=[out, accum_out])

    def ts(self, out, in0, s1, s2, op0, op1=None, eng="dve", accum_out=None):
        eng = "dve"
        kw = {}
        if op1 is not None:
            kw["op1"] = op1
        if accum_out is not None:
            kw["accum_out"] = accum_out
        self.op(eng, lambda e: e.tensor_scalar(out, in0, s1, s2, op0, **kw), r=[in0, s1, s2], w=[out, accum_out])

    def tt(self, out, in0, in1, op, eng="dve"):
        self.op(eng, lambda e: e.tensor_tensor(out, in0, in1, op), r=[in0, in1], w=[out])

    def stt(self, out, in0, scalar, in1, op0, op1, eng="dve"):
        eng = "dve"
        self.op(eng, lambda e: e.scalar_tensor_tensor(out, in0, scalar, in1, op0, op1), r=[in0, scalar, in1], w=[out])

    def copy(self, out, in_, eng="dve"):
        if eng == "act":
            self.op(eng, lambda e: e.copy(out, in_), r=[in_], w=[out])
        else:
            self.op(eng, lambda e: e.tensor_copy(out, in_), r=[in_], w=[out])

    def memset(self, out, val, eng="dve"):
        self.op(eng, lambda e: e.memset(out, val), w=[out])

    def recip(self, out, in_):
        self.op("dve", lambda e: e.reciprocal(out, in_), r=[in_], w=[out])

    def reduce(self, out, in_, op, axis=AX.X, eng="dve"):
        self.op(eng, lambda e: e.tensor_reduce(out, in_, axis, op), r=[in_], w=[out])


def _layout(items):
    off = {}
    o = 0
    for name, n in items:
        off[name] = (o, n)
        o += n
    return off, o


VFM, NVFM = _layout((("g1", 8), ("g2", 8), ("adab", 48), ("caw", 124), ("cab", 4), ("lag", 4), ("lab", 4),
                     ("qng", 1), ("kng", 1), ("ccw", 48), ("fcw", 132), ("fcb", 44)))
VBC, NVBC = _layout((("onorm", 512), ("alog", 8), ("dtb", 8)))
GCON, NGCON = _layout((("tri0", 64), ("tri1", 64), ("ntri0", 64), ("ntri1", 64), ("bigd0", 256), ("bigd1", 256),
                       ("nstr0", 256), ("nstr1", 256), ("iden4", 256)))
NCASE = 12
LIN_TILES = [(0, 256)] + [(256 + 512 * i, 512) for i in range(8)]
SEGS = [(0, NCTX), (NCTX, NLAT)]


def na_plan():
    plan = []
    for r0 in range(0, 64, 2):
        if r0 == 0:
            plan.append((0, 4, 3))
        elif r0 == 2:
            plan.append((0, 4, 2))
        elif r0 == 60:
            plan.append((56, 4, 1))
        elif r0 == 62:
            plan.append((56, 4, 0))
        else:
            plan.append((r0 - 4, 5, 7))
    return plan


def build(dbg=None):
    dbg = dbg or {}
    NL = dbg.get("nl", DEPTH)
    stop = dbg.get("stop")
    nc = bass.Bass("TRN2", target_bir_lowering=False)
    es = ExitStack()
    P = Prog(nc, es)
    P.psum_init()

    def din(name, shape, dt=F32):
        return nc.dram_tensor(name, list(shape), dt, kind="ExternalInput").ap()

    def dscr(name, shape, dt=F32):
        kind = "ExternalOutput" if name in dbg else "Internal"
        return nc.dram_tensor(name, list(shape), dt, kind=kind).ap()

    x_in = din("x", [NLAT, D])
    ctx_in = din("ctx", [NCTX, D])
    cvec = din("cvec", [128, 16])
    ada_w = din("ada_w", [NL, 128, 8, 6 * D])
    vfm_d = din("vfm", [NL, 128, NVFM])
    vbc_d = din("vbc", [NL, 128, NVBC])
    w_in = din("w_in", [NL, 128, 8, NIN])
    nab_d = din("nab", [NL, 128, NCASE * 8 * 128])
    w_br = din("w_br", [NL, 128, 12, D])
    w_out = din("w_out", [NL, 128, 8, D])
    w_up = din("w_up", [NL, 128, 8, 2 * DFF])
    w_dn = din("w_dn", [NL, 128, 22, D])
    ident_d = din("ident", [128, 128])
    rm_d = din("rm", [128, 128])
    gcon_d = din("gcon", [128, NGCON])
    ropec_d = din("ropec", [128, NLAT])
    ropes_d = din("ropes", [128, NLAT])
    y_out = nc.dram_tensor("y", [NLAT, D], F32, kind="ExternalOutput").ap()

    YAP = DT(dscr("YAP", [512, TALL]), 1, TALL)
    QN = DT(dscr("QN", [512, TALL], BF16), 1, TALL)
    KN = DT(dscr("KN", [512, TALL], BF16), 1, TALL)
    VB = DT(dscr("VB", [TALL, 520], BF16), 0, TALL)
    CQ = DT(dscr("CQ", [1536, TALL]), 1, TALL)
    OGS = DT(dscr("OGS", [TALL, 512]), 0, TALL)
    GB = DT(dscr("GB", [TALL, 16]), 0, TALL)
    MG = DT(dscr("MG", [3072, TALL], BF16), 1, TALL)
    YA = DT(dscr("YA", [512, TALL], BF16), 1, TALL)
    YB = DT(dscr("YB", [512, TALL], BF16), 1, TALL)
    YC = DT(dscr("YC", [512, TALL], BF16), 1, TALL)
    QF = DT(dscr("QF", [512, TALL]), 1, TALL)
    KF = DT(dscr("KF", [512, TALL]), 1, TALL)
    KT = DT(dscr("KT", [TALL, 512]), 0, TALL)
    VT = DT(dscr("VT", [TALL, 512]), 0, TALL)
    OD = [DT(dscr(f"OD{d}", [TALL, 512]), 0, TALL) for d in range(2)]
    ACTT = DT(dscr("ACTT", [DFF, TALL], BF16), 1, TALL)
    X1 = DT(dscr("X1", [NLAT, D]), 0, NLAT)
    X2 = DT(dscr("X2", [NLAT, D]), 0, NLAT)
    C1 = DT(dscr("C1", [NCTX, D]), 0, NCTX)
    C2 = DT(dscr("C2", [NCTX, D]), 0, NCTX)
    MOD = DT(dscr("MOD", [128, 96]), 1, 128)
    HT = DT(dscr("HT", [D, TALL], BF16), 1, TALL)

    ident = P.buf("ident", [128, 128], F32)
    rm = P.buf("rm", [128, 128], F32)
    ones = P.buf("ones", [128, 128], F32)
    bones = P.buf("bones", [128, 128], F32)
    o512 = P.buf("o512", [128, 128], F32)
    gcon = P.buf("gcon", [128, NGCON], F32)
    sc = P.buf("sc", [128, 16], F32)
    vfm = P.buf("vfm", [128, NVFM], F32)
    vbc = P.buf("vbc", [128, NVBC], F32)
    mfm = P.buf("mfm", [128, 96], F32)
    aff = P.buf("aff", [128, 32], F32)
    sm = P.buf("sm", [128, 64], F32)
    gbc = [[P.buf(f"gbc{k}{j}", [128, D], F32) for j in range(2)] for k in range(2)]
    rep = [P.buf(f"rep{i}", [128, 128], F32) for i in range(2)]
    P.arena_init(168 * 1024)

    def vf(name, i=0, n=1):
        o, _ = VFM[name]
        return vfm[:, o + i:o + i + n]

    def vb_(name, i=0, n=None):
        o, nn = VBC[name]
        return vbc[:, o + i:o + i + (n or nn)]

    def gc(name, rows=64):
        o, n = GCON[name]
        return gcon[0:rows, o:o + n]

    P.dma("sp", ident[:, :], ident_d[:, :])
    P.dma("sp", rm[:, :], rm_d[:, :])
    P.dma("sp", gcon[:, :], gcon_d[:, :])
    P.dma("sp", sc[:, :], cvec[:, :])
    P.memset(ones[:, :], 1.0)
    P.memset(o512[:, :], 1.0 / 512)
    P.memset(bones[:, :], 0.0)
    P.memset(bones[0:64, 0:64], 1.0 / 64)
    P.memset(bones[64:128, 64:128], 1.0 / 64)
    P.act(sc[:, :], sc[:, :], AF.Silu)

    rr = {"rep": 0}

    def bcast_row(dst, src_cols):
        for c0 in range(0, len(src_cols), 4):
            ps = P.ps()
            grp = src_cols[c0:c0 + 4]
            for i, col in enumerate(grp):
                rp = rep[rr["rep"] % 2]
                rr["rep"] += 1
                P.ts(rp[:, :], ones[:, :], col, None, ALU.mult)
                P.mm(ps[:, i * 128:(i + 1) * 128], lhsT=rp[:, :], rhs=ident[:, :])
            n = len(grp) * 128
            P.copy(dst[:, c0 * 128:c0 * 128 + n], ps[:, 0:n], eng="act")

    class Rot:
        def __init__(self, bufs):
            self.b = bufs
            self.i = 0

        def __call__(self):
            b = self.b[self.i % len(self.b)]
            self.i += 1
            return b

    def norm_tile(src_ap, src_tks, dst_view, k, j, xt, xn, s):
        P.dma("sp", xt[:, :], src_ap, r=src_tks)
        P.memset(s[:, 0:1], 0.0)
        P.act(xn[:, :], xt[:, :], AF.Square, accum_out=s[:, 0:1])
        P.act(s[:, 1:2], s[:, 0:1], AF.Sqrt, scale=1.0 / D, bias=EPS)
        P.recip(s[:, 2:3], s[:, 1:2])
        P.act(xn[:, :], xt[:, :], AF.Copy, scale=s[:, 2:3])
        bi = 0 if k == 0 else 3
        for half in range(2):
            ps = P.ps()
            for i in range(4):
                c = half * 4 + i
                P.tr(ps[:, i * 128:(i + 1) * 128], xn[:, c * 128:(c + 1) * 128], ident[:, :])
            for i in range(4):
                c = half * 4 + i
                a = aff[:, k * 16 + c * 2 + j:k * 16 + c * 2 + j + 1]
                b = mfm[:, (bi * 8 + c) * 2 + j:(bi * 8 + c) * 2 + j + 1]
                if i % 2 == 0:
                    P.ts(dst_view(c), ps[:, i * 128:(i + 1) * 128], a, b, ALU.mult, ALU.add)
                else:
                    P.act(dst_view(c), ps[:, i * 128:(i + 1) * 128], AF.Identity, bias=b, scale=a)

    def src_of(t0, x_src, c_src):
        if t0 < NCTX:
            return c_src.ap[t0:t0 + 128, :], c_src.tk(t0, 128), 1
        return x_src.ap[t0 - NCTX:t0 - NCTX + 128, :], x_src.tk(t0 - NCTX, 128), 0

    def layer(l, x_src, c_src, x1_dst, c1_dst, x2_dst, c2_dst, last):
        P.arena_reset()
        P.dma("sp", vfm[:, :], vfm_d[l])
        P.dma("sp", vbc[:, :], vbc_d[l])
        aws = [P.av(8 * 1024), P.av(8 * 1024)]
        for blk in range(6):
            aw = aws[blk % 2]
            P.dma("sp", aw[:, :].rearrange("p (k n) -> p k n", k=8), ada_w[l, :, :, blk * 1024:(blk + 1) * 1024])
            for n_ in range(8):
                idx = blk * 8 + n_
                ps = P.ps()
                for kc in range(8):
                    P.mm(ps[:, 0:2], lhsT=aw[:, kc * 1024 + n_ * 128: kc * 1024 + (n_ + 1) * 128],
                         rhs=sc[:, kc * 2:kc * 2 + 2], start=(kc == 0), stop=(kc == 7))
                P.ts(mfm[:, idx * 2:idx * 2 + 2], ps[:, 0:2], vf("adab", idx), None, ALU.add)
        if "MOD" in dbg:
            P.dma("pool", MOD.ap[:, :], mfm[:, :], w=MOD.tk(0, 128))
        for k, (gname, mi) in enumerate((("g1", 1), ("g2", 4))):
            for c in range(8):
                P.ts(aff[:, k * 16 + c * 2:k * 16 + c * 2 + 2], mfm[:, (mi * 8 + c) * 2:(mi * 8 + c) * 2 + 2],
                     1.0, vf(gname, c), ALU.add, ALU.mult)
        for k, mi in enumerate((2, 5)):
            for j in range(2):
                bcast_row(gbc[k][j], [mfm[:, (mi * 8 + c) * 2 + j:(mi * 8 + c) * 2 + j + 1] for c in range(8)])
        P.ts(sm[:, 0:1], vf("qng"), 0.125, None, ALU.mult)
        P.act(sm[:, 8:16], vb_("alog"), AF.Exp)
        P.ts(sm[:, 8:16], sm[:, 8:16], -1.0, None, ALU.mult)

        P.arena_reset()
        hT = P.av(8 * TALL, BF16)
        WB = [P.av(8 * 512, BF16) for _ in range(2)]
        XT = [P.av(D) for _ in range(2)]
        XN = [P.av(D) for _ in range(2)]
        ST = [P.av(8) for _ in range(2)]
        wk = Rot([P.av(512) for _ in range(6)])
        wkh = Rot([P.av(520, BF16) for _ in range(4)])
        t16 = Rot([P.av(48) for _ in range(3)])
        for ti in range(TALL // 128):
            t0 = ti * 128
            ap, tks, j = src_of(t0, x_src, c_src)
            norm_tile(ap, tks, lambda c, t0=t0: hT[:, c * TALL + t0:c * TALL + t0 + 128], 0, j,
                      XT[ti % 2], XN[ti % 2], ST[ti % 2])
        if "HT" in dbg:
            P.dma("pool", HT.ap.rearrange("(c p) t -> p c t", p=128), hT[:, :].rearrange("p (c t) -> p c t", c=8),
                  w=HT.tk(0, TALL))
        if stop == "p1a":
            return True

        def hview(kc, t0, n):
            return hT[:, kc * TALL + t0:kc * TALL + t0 + n]

        def epi_glu(i):
            def f(pss, t0, nt):
                sg = wk()
                P.act(sg[:, :nt], pss[1][:, :nt], AF.Sigmoid)
                yv = wk()
                P.tt(yv[:, :nt], pss[0][:, :nt], sg[:, :nt], ALU.mult)
                P.dma("pool", YAP.ap[i * 128:(i + 1) * 128, t0:t0 + nt], yv[:, :nt], w=YAP.tk(t0, nt))
            return f

        def epi_qk(i, dst, gcol):
            def f(pss, t0, nt):
                sq = wk()
                P.act(sq[:, :nt], pss[0][:, :nt], AF.Square)
                ps2 = P.ps()
                P.mm(ps2[:, :nt], lhsT=bones[:, :], rhs=sq[:, :nt])
                rs = wk()
                P.act(rs[:, :nt], ps2[:, :nt], AF.Sqrt, bias=EPS)
                P.recip(rs[:, :nt], rs[:, :nt])
                qn = wkh()
                P.stt(qn[:, :nt], pss[0][:, :nt], gcol, rs[:, :nt], ALU.mult, ALU.mult)
                P.dma("pool", dst.ap[i * 128:(i + 1) * 128, t0:t0 + nt], qn[:, :nt], w=dst.tk(t0, nt))
            return f

        def epi_v(pss, t0, nt):
            vh = wkh()
            P.memset(vh[:, 0:520], 1.0)
            P.copy(vh[:, 0:520].rearrange("p (h e) -> p h e", h=8)[:, :, 0:64],
                   pss[0][:, 0:512].rearrange("p (h e) -> p h e", h=8), eng="act")
            P.dma("pool", VB.ap[t0:t0 + 128, :], vh[:, 0:520], w=VB.tk(t0, 128))

        def epi_copy(i, dst):
            def f(pss, t0, nt):
                fv = wk()
                P.copy(fv[:, :nt], pss[0][:, :nt], eng=("act" if i % 2 else "dve"))
                P.dma("pool", dst.ap[i * 128:(i + 1) * 128, t0:t0 + nt], fv[:, :nt], w=dst.tk(t0, nt))
            return f

        def epi_og(pss, t0, nt):
            fv = wk()
            P.act(fv[:, :], pss[0][:, 0:512], AF.Silu)
            P.dma("pool", OGS.ap[t0:t0 + 128, :], fv[:, :], w=OGS.tk(t0, 128))

        def epi_db(pss, t0, nt):
            t = t16()
            u, z, z2, pl, o = t[:, 0:8], t[:, 8:16], t[:, 16:24], t[:, 24:32], t[:, 32:48]
            P.tt(u, pss[0][:, 0:8], vb_("dtb"), ALU.add)
            P.act(u, u, AF.Exp)
            P.ts(z, u, 2.0, None, ALU.add)
            P.recip(z, z)
            P.tt(z, z, u, ALU.mult)
            P.tt(z2, z, z, ALU.mult)
            P.ts(pl, z2, 1.0 / 11, 1.0 / 9, ALU.mult, ALU.add)
            for cf in (1.0 / 7, 1.0 / 5, 1.0 / 3, 1.0):
                P.tt(pl, pl, z2, ALU.mult)
                P.ts(pl, pl, cf, None, ALU.add)
            P.tt(pl, pl, z, ALU.mult)
            P.stt(o[:, 0:8], pl, 2.0, sm[:, 8:16], ALU.mult, ALU.mult)
            P.act(o[:, 8:16], pss[0][:, 8:16], AF.Sigmoid)
            P.dma("pool", GB.ap[t0:t0 + 128, :], o, w=GB.tk(t0, 128))

        def epi_mg(i):
            def f(pss, t0, nt):
                hv = wkh()
                P.act(hv[:, :nt], pss[0][:, :nt], AF.Sigmoid)
                P.dma("pool", MG.ap[i * 128:(i + 1) * 128, t0:t0 + nt], hv[:, :nt], w=MG.tk(t0, nt))
            return f

        units = []
        for i in range(4):
            units.append(("fm", [(i * 128, 128), (512 + i * 128, 128)], epi_glu(i)))
        for i in range(4):
            units.append(("fm", [(1024 + i * 128, 128)], epi_qk(i, QN, sm[:, 0:1])))
        for i in range(4):
            units.append(("fm", [(1536 + i * 128, 128)], epi_qk(i, KN, vf("kng"))))
        units.append(("tm", [(2048, 512)], epi_v))
        for i in range(12):
            units.append(("fm", [(2560 + i * 128, 128)], epi_copy(i, CQ)))
        units.append(("tm", [(4096, 512)], epi_og))
        units.append(("tm", [(4608, 16)], epi_db))
        for i in range(24):
            units.append(("fm", [(4624 + i * 128, 128)], epi_mg(i)))

        def linear(units, wdram, KC, xview, tiles, WBs):
            def load(ui):
                mode, ranges, _ = units[ui]
                wb = WBs[ui % 2]
                ncols = sum(n for _, n in ranges)
                wv = wb[:, 0:KC * ncols].rearrange("p (k n) -> p k n", k=KC)
                off = 0
                for (c0, n) in ranges:
                    P.dma("pool", wv[:, :, off:off + n], wdram[:, :, c0:c0 + n])
                    off += n
            load(0)
            for ui, (mode, ranges, epi) in enumerate(units):
                if ui + 1 < len(units):
                    load(ui + 1)
                wb = WBs[ui % 2]
                ncols = sum(n for _, n in ranges)
                for (t0, nt) in tiles:
                    if mode == "fm":
                        pss = []
                        off = 0
                        for (c0, n) in ranges:
                            ps = P.ps()
                            for kc in range(KC):
                                P.mm(ps[0:n, 0:nt], lhsT=wb[:, kc * ncols + off:kc * ncols + off + n],
                                     rhs=xview(kc, t0, nt), start=(kc == 0), stop=(kc == KC - 1))
                            pss.append(ps)
                            off += n
                        epi(pss, t0, nt)
                    else:
                        for s0 in range(0, nt, 128):
                            ps = P.ps()
                            for kc in range(KC):
                                P.mm(ps[:, 0:ncols], lhsT=xview(kc, t0 + s0, 128),
                                     rhs=wb[:, kc * ncols:(kc + 1) * ncols], start=(kc == 0), stop=(kc == KC - 1))
                            epi([ps], t0 + s0, 128)

        linear(units, w_in[l], 8, hview, LIN_TILES, WB)
        if stop == "p1b":
            return True

        P.arena_reset()
        YBs = [P.av(544) for _ in range(4)]
        CAs = [P.av(512) for _ in range(4)]
        wk = Rot([P.av(512) for _ in range(4)])
        wkh = Rot([P.av(512, BF16) for _ in range(4)])
        segs2 = SEGS if not last else SEGS[1:]
        for (s0, sn) in segs2:
            for t0 in range(s0, s0 + sn, 512):
                nt = min(512, s0 + sn - t0)
                lo, hi = max(t0 - 15, s0), min(t0 + nt + 15, s0 + sn)
                for c in range(4):
                    yb = YBs[c]
                    eng = "dve" if c % 2 == 0 else "pool"
                    if lo > t0 - 15 or hi < t0 + nt + 15:
                        P.memset(yb[:, :], 0.0, eng=eng)
                    P.dma("sp", yb[:, lo - (t0 - 15):hi - (t0 - 15)], YAP.ap[c * 128:(c + 1) * 128, lo:hi],
                          r=YAP.tk(lo, hi - lo))
                    acc = CAs[c]
                    o, _ = VFM["caw"]
                    P.ts(acc[:, :nt], yb[:, 0:nt], vfm[:, o + c * 31:o + c * 31 + 1], vf("cab", c), ALU.mult, ALU.add,
                         eng=eng)
                    for tap in range(1, 31):
                        P.stt(acc[:, :nt], yb[:, tap:tap + nt], vfm[:, o + c * 31 + tap:o + c * 31 + tap + 1],
                              acc[:, :nt], ALU.mult, ALU.add, eng=eng)
                    P.act(yb[:, :nt], acc[:, :nt], AF.Square)
                psm, psq = P.ps(), P.ps()
                for c in range(4):
                    P.mm(psm[:, :nt], lhsT=o512[:, :], rhs=CAs[c][:, :nt], start=(c == 0), stop=(c == 3))
                for c in range(4):
                    P.mm(psq[:, :nt], lhsT=o512[:, :], rhs=YBs[c][:, :nt], start=(c == 0), stop=(c == 3))
                msq, var = wk(), wk()
                P.act(msq[:, :nt], psm[:, :nt], AF.Square)
                P.tt(var[:, :nt], psq[:, :nt], msq[:, :nt], ALU.subtract)
                P.act(var[:, :nt], var[:, :nt], AF.Sqrt, bias=EPS)
                P.recip(var[:, :nt], var[:, :nt])
                for c in range(4):
                    acc = CAs[c]
                    P.tt(acc[:, :nt], acc[:, :nt], psm[:, :nt], ALU.subtract)
                    P.tt(acc[:, :nt], acc[:, :nt], var[:, :nt], ALU.mult)
                    yh = wkh()
                    P.act(yh[:, :nt], acc[:, :nt], AF.Silu, scale=vf("lag", c), bias=vf("lab", c))
                    P.dma("pool", YA.ap[c * 128:(c + 1) * 128, t0:t0 + nt], yh[:, :nt], w=YA.tk(t0, nt))
        if stop == "p2":
            return True

        P.arena_reset()
        nab = P.av(NCASE * 8 * 128)
        P.dma("sp", nab[:, :], nab_d[l])
        kc_ = P.av(4 * 256, BF16)
        vc_ = P.av(2 * 520, BF16)
        P.dma("sp", kc_[:, :].rearrange("p (c t) -> p c t", c=4), KN.ap[:, 0:256].rearrange("(c p) t -> p c t", p=128),
              r=KN.tk(0, 256))
        P.dma("sp", vc_[:, :].rearrange("p (j e) -> p j e", j=2), VB.ap[0:256, :].rearrange("(j p) e -> p j e", p=128),
              r=VB.tk(0, 256))
        QTs = [P.av(4 * 128, BF16) for _ in range(2)]
        KLs = [P.av(4 * 640, BF16) for _ in range(2)]
        VLs = [P.av(5 * 520, BF16) for _ in range(2)]
        SBs = Rot([P.av(640) for _ in range(2)])
        PTs = Rot([P.av(896, BF16) for _ in range(3)])
        ybs = Rot([P.av(512) for _ in range(2)])
        rcs = Rot([P.av(8) for _ in range(2)])
        ybTs = Rot([P.av(512, BF16) for _ in range(2)])
        plan = na_plan()
        P.psn, P.psi = 6, 0
        qtiles = [] if last else [("c", 0), ("c", 128)]
        qtiles += [("l", i) for i in range(32)]
        for qi, (kind, qv) in enumerate(qtiles):
            qT, kL, vL = QTs[qi % 2], KLs[qi % 2], VLs[qi % 2]
            if kind == "c":
                tq0, nk, case0 = qv, 0, 0
            else:
                tq0 = NCTX + qv * 128
                kr0, nk, case0 = plan[qv]
                kt0 = NCTX + kr0 * 64
                P.dma("sp", kL[:, 0:4 * nk * 128].rearrange("p (c t) -> p c t", c=4),
                      KN.ap[:, kt0:kt0 + nk * 128].rearrange("(c p) t -> p c t", p=128), r=KN.tk(kt0, nk * 128))
                P.dma("sp", vL[:, 0:nk * 520].rearrange("p (j e) -> p j e", j=nk),
                      VB.ap[kt0:kt0 + nk * 128, :].rearrange("(j p) e -> p j e", p=128), r=VB.tk(kt0, nk * 128))
            P.dma("sp", qT[:, :].rearrange("p (c t) -> p c t", c=4),
                  QN.ap[:, tq0:tq0 + 128].rearrange("(c p) t -> p c t", p=128), r=QN.tk(tq0, 128))
            pos = [P.psb[6], P.psb[7]]
            for h in range(8):
                c, hf = h // 2, (h % 2) * 64
                qv_ = qT[hf:hf + 64, c * 128:(c + 1) * 128]
                blocks = [("l", j) for j in range(nk)] + [("c", 0), ("c", 1)]
                psA, psB = P.ps(), P.ps()
                for bi, (bk, j) in enumerate(blocks):
                    dst = psA[:, bi * 128:(bi + 1) * 128] if bi < 4 else psB[:, (bi - 4) * 128:(bi - 3) * 128]
                    if bk == "l":
                        kv = kL[hf:hf + 64, c * nk * 128 + j * 128:c * nk * 128 + (j + 1) * 128]
                    else:
                        kv = kc_[hf:hf + 64, c * 256 + j * 128:c * 256 + (j + 1) * 128]
                    P.mm(dst, lhsT=kv, rhs=qv_)
                pt = PTs()
                nb = len(blocks)
                if nk > 0:
                    sb = SBs()
                    na = min(nk, 4)
                    bview = nab[:, :].rearrange("p (k h q) -> p k h q", k=NCASE, h=8)[:, case0:case0 + na, h, :]
                    P.tt(sb[:, 0:na * 128].rearrange("p (k q) -> p k q", k=na),
                         psA[:, 0:na * 128].rearrange("p (k q) -> p k q", k=na), bview, ALU.add)
                    if nk == 5:
                        P.tt(sb[:, 512:640], psB[:, 0:128],
                             nab[:, ((case0 + 4) * 8 + h) * 128:((case0 + 4) * 8 + h + 1) * 128], ALU.add)
                    P.act(pt[:, 0:nk * 128], sb[:, 0:nk * 128], AF.Exp)
                cb0 = nk
                if cb0 < 4:
                    na_c = min(2, 4 - cb0)
                    P.act(pt[:, cb0 * 128:(cb0 + na_c) * 128], psA[:, cb0 * 128:(cb0 + na_c) * 128], AF.Exp)
                    if na_c < 2:
                        P.act(pt[:, (cb0 + na_c) * 128:(cb0 + 2) * 128], psB[:, 0:(2 - na_c) * 128], AF.Exp)
                else:
                    P.act(pt[:, cb0 * 128:(cb0 + 2) * 128], psB[:, (cb0 - 4) * 128:(cb0 - 2) * 128], AF.Exp)
                po = pos[h // 4]
                for bi, (bk, j) in enumerate(blocks):
                    if bk == "l":
                        vv = vL[:, j * 520 + h * 65:j * 520 + (h + 1) * 65]
                    else:
                        vv = vc_[:, j * 520 + h * 65:j * 520 + (h + 1) * 65]
                    P.mm(po[:, (h % 4) * 65:(h % 4 + 1) * 65], lhsT=pt[:, bi * 128:(bi + 1) * 128], rhs=vv,
                         start=(bi == 0), stop=(bi == nb - 1))
            rc, ybt = rcs(), ybs()
            for g in range(2):
                pv = pos[g][:, 0:260].rearrange("p (h e) -> p h e", h=4)
                P.recip(rc[:, g * 4:(g + 1) * 4].unsqueeze(2), pv[:, :, 64:65])
                P.tt(ybt[:, g * 256:(g + 1) * 256].rearrange("p (h e) -> p h e", h=4), pv[:, :, 0:64],
                     rc[:, g * 4:(g + 1) * 4].unsqueeze(2).to_broadcast([128, 4, 64]), ALU.mult)
            ps = P.ps()
            for c in range(4):
                P.tr(ps[:, c * 128:(c + 1) * 128], ybt[:, c * 128:(c + 1) * 128], ident[:, :])
            yT = ybTs()
            P.copy(yT[:, :], ps[:, :], eng="act")
            P.dma("pool", YB.ap[:, tq0:tq0 + 128].rearrange("(c p) t -> p c t", p=128),
                  yT[:, :].rearrange("p (c t) -> p c t", c=4), w=YB.tk(tq0, 128))
        P.psn = 8
        if stop == "p3":
            return True

        P.arena_reset()
        ybq = Rot([P.av(520) for _ in range(3)])
        wk = Rot([P.av(512) for _ in range(8)])
        cs_ = [P.av(512) for _ in range(2)]
        sn_ = [P.av(512) for _ in range(2)]
        ti_ = 0
        for (s0, sn) in SEGS:
            for t0 in range(s0, s0 + sn, 512):
                nt = min(512, s0 + sn - t0)
                lo, hi = max(t0 - 2, s0), min(t0 + nt + 1, s0 + sn)
                rope = s0 == NCTX
                if rope:
                    cs, sn2 = cs_[ti_ % 2], sn_[ti_ % 2]
                    ti_ += 1
                    P.dma("sp", cs[:, :nt], ropec_d[:, t0 - NCTX:t0 - NCTX + nt])
                    P.dma("sp", sn2[:, :nt], ropes_d[:, t0 - NCTX:t0 - NCTX + nt])
                for ch in range(12):
                    yb = ybq()
                    eng = "dve" if ch % 2 == 0 else "pool"
                    if lo > t0 - 2 or hi < t0 + nt + 1:
                        P.memset(yb[:, :], 0.0, eng=eng)
                    P.dma("sp", yb[:, lo - (t0 - 2):hi - (t0 - 2)], CQ.ap[ch * 128:(ch + 1) * 128, lo:hi],
                          r=CQ.tk(lo, hi - lo))
                    acc = wk()
                    o, _ = VFM["ccw"]
                    P.ts(acc[:, :nt], yb[:, 0:nt], vfm[:, o + ch * 4:o + ch * 4 + 1], None, ALU.mult, eng=eng)
                    for tap in range(1, 4):
                        P.stt(acc[:, :nt], yb[:, tap:tap + nt], vfm[:, o + ch * 4 + tap:o + ch * 4 + tap + 1],
                              acc[:, :nt], ALU.mult, ALU.add, eng=eng)
                    y = wk()
                    P.act(y[:, :nt], acc[:, :nt], AF.Silu)
                    if ch < 8:
                        sq = wk()
                        P.act(sq[:, :nt], y[:, :nt], AF.Square)
                        ps = P.ps()
                        P.mm(ps[:, :nt], lhsT=ones[:, :], rhs=sq[:, :nt])
                        rn = wk()
                        P.act(rn[:, :nt], ps[:, :nt], AF.Sqrt, bias=EPS)
                        P.recip(rn[:, :nt], rn[:, :nt])
                        if ch < 4:
                            P.stt(y[:, :nt], y[:, :nt], 128.0 ** -0.5, rn[:, :nt], ALU.mult, ALU.mult)
                        else:
                            P.tt(y[:, :nt], y[:, :nt], rn[:, :nt], ALU.mult)
                        if rope:
                            ps2 = P.ps()
                            P.mm(ps2[:, :nt], lhsT=rm[:, :], rhs=y[:, :nt])
                            r2 = wk()
                            P.tt(r2[:, :nt], ps2[:, :nt], sn2[:, :nt], ALU.mult)
                            P.tt(y[:, :nt], y[:, :nt], cs[:, :nt], ALU.mult)
                            P.tt(y[:, :nt], y[:, :nt], r2[:, :nt], ALU.add)
                        dst = QF if ch < 4 else KF
                        hh = ch % 4
                        P.dma("pool", dst.ap[hh * 128:(hh + 1) * 128, t0:t0 + nt], y[:, :nt], w=dst.tk(t0, nt))
                    if ch >= 4:
                        dstt = KT if ch < 8 else VT
                        hh = ch % 4
                        ps3 = P.ps()
                        for s_ in range(nt // 128):
                            P.tr(ps3[:, s_ * 128:(s_ + 1) * 128], y[:, s_ * 128:(s_ + 1) * 128], ident[:, :])
                        tt_ = wk()
                        P.copy(tt_[:, :nt], ps3[:, :nt], eng="act")
                        P.dma("pool", dstt.ap[t0:t0 + nt, hh * 128:(hh + 1) * 128].rearrange("(s p) e -> p s e", p=128),
                              tt_[:, :nt].rearrange("p (s e) -> p s e", e=128), w=dstt.tk(t0, nt))
        if stop == "p4a":
            return True

        P.arena_reset()
        NB = 3
        kfs = [P.av(256) for _ in range(NB)]
        qfs = [P.av(256) for _ in range(NB)]
        kts = [P.av(512) for _ in range(NB)]
        vts = [P.av(512) for _ in range(NB)]
        gbs = [P.av(16) for _ in range(NB)]
        S = [P.av(512) for _ in range(2)]
        m64 = Rot([P.av(256) for _ in range(24)])
        m512 = Rot([P.av(512) for _ in range(8)])
        sm4 = Rot([P.av(16) for _ in range(8)])
        for d in range(2):
            P.memset(S[d][:, :], 0.0)
        fwd_order = list(range(TALL // 64))
        bwd_order = [3, 2, 1, 0] + list(range(TALL // 64 - 1, 3, -1))
        loaded = {}

        def load_chunk(ci, slot):
            cs0 = ci * 64
            P.dma("sp", kfs[slot][:, :].rearrange("p (h t) -> p h t", h=4),
                  KF.ap[:, cs0:cs0 + 64].rearrange("(h p) t -> p h t", p=128), r=KF.tk(cs0, 64))
            P.dma("sp", qfs[slot][:, :].rearrange("p (h t) -> p h t", h=4),
                  QF.ap[:, cs0:cs0 + 64].rearrange("(h p) t -> p h t", p=128), r=QF.tk(cs0, 64))
            P.dma("sp", kts[slot][0:64, :], KT.ap[cs0:cs0 + 64, :], r=KT.tk(cs0, 64))
            P.dma("sp", vts[slot][0:64, :], VT.ap[cs0:cs0 + 64, :], r=VT.tk(cs0, 64))
            P.dma("sp", gbs[slot][0:64, :], GB.ap[cs0:cs0 + 64, :], r=GB.tk(cs0, 64))

        def v4(b, rows=64):
            return b[0:rows, 0:256]

        def h3(ap):
            return ap.rearrange("p (h n) -> p h n", h=4)

        def bc4(col4, rows=64, n=64):
            return col4.unsqueeze(2).to_broadcast([rows, 4, n])

        def chunk(ci, d, slot):
            cs0 = ci * 64
            kf, qf, kt, vt, gb = kfs[slot], qfs[slot], kts[slot], vts[slot], gbs[slot]
            g4 = gb[0:64, d * 4:(d + 1) * 4]
            be4 = gb[0:64, 8 + d * 4:8 + (d + 1) * 4]
            tri, ntri = gc(f"tri{d}"), gc(f"ntri{d}")
            grep = m64()
            P.copy(h3(v4(grep)), bc4(g4))
            psg = P.ps()
            P.mm(psg[0:64, 0:4], lhsT=tri, rhs=g4)
            P.mm(psg[:, 8:12], lhsT=ones[0:64, :], rhs=g4)
            sv = sm4()
            gcol, eg, bw, e2, egl = sv[0:64, 0:4], sv[0:64, 4:8], sv[0:64, 8:12], sv[0:64, 12:16], None
            sv2 = sm4()
            egl = sv2[:, 0:4]
            P.copy(gcol, psg[0:64, 0:4])
            P.act(eg, psg[0:64, 0:4], AF.Exp)
            P.tt(bw, eg, be4, ALU.mult)
            P.tt(e2, psg[0:64, 8:12], gcol, ALU.subtract)
            P.act(e2, e2, AF.Exp)
            P.act(egl, psg[:, 8:12], AF.Exp)
            psE = P.ps()
            for h in range(4):
                P.mm(psE[0:64, h * 64:(h + 1) * 64], lhsT=grep[0:64, h * 64:(h + 1) * 64], rhs=tri, start=True, stop=False,
                     signal=False)
                P.mm(psE[0:64, h * 64:(h + 1) * 64], lhsT=ntri, rhs=grep[0:64, h * 64:(h + 1) * 64], start=False, stop=True)
            dec, decT = m64(), m64()
            P.tt(v4(dec), psE[0:64, 0:256], gc(f"bigd{d}"), ALU.add)
            P.act(v4(dec), v4(dec), AF.Exp, scale=-1.0)
            P.tt(v4(decT), psE[0:64, 0:256], gc(f"bigd{1 - d}"), ALU.subtract)
            P.act(v4(decT), v4(decT), AF.Exp)
            psK, psQ = P.ps(), P.ps()
            for h in range(4):
                P.mm(psK[0:64, h * 64:(h + 1) * 64], lhsT=kf[:, h * 64:(h + 1) * 64], rhs=kf[:, h * 64:(h + 1) * 64])
            for h in range(4):
                P.mm(psQ[0:64, h * 64:(h + 1) * 64], lhsT=kf[:, h * 64:(h + 1) * 64], rhs=qf[:, h * 64:(h + 1) * 64])
            AT = m64()
            P.tt(v4(AT), psQ[0:64, 0:256], v4(decT), ALU.mult)
            A, B = m64(), m64()
            P.tt(v4(A), psK[0:64, 0:256], v4(dec), ALU.mult)
            P.tt(v4(A), v4(A), gc(f"nstr{d}"), ALU.mult)
            P.tt(h3(v4(A)), h3(v4(A)), bc4(be4), ALU.mult)
            psT = P.ps()
            for h in range(4):
                P.tr(psT[0:64, h * 64:(h + 1) * 64], A[0:64, h * 64:(h + 1) * 64], ident[0:64, 0:64])
            P.copy(v4(B), psT[0:64, 0:256], eng="act")
            Y = m64()
            P.tt(v4(Y), v4(B), gc("iden4"), ALU.add)
            for lev in range(5):
                psA2, psB2 = P.ps(), P.ps()
                for h in range(4):
                    sl = slice(h * 64, (h + 1) * 64)
                    P.mm(psA2[0:64, sl], lhsT=B[0:64, sl], rhs=A[0:64, sl])
                for h in range(4):
                    sl = slice(h * 64, (h + 1) * 64)
                    P.mm(psB2[0:64, sl], lhsT=A[0:64, sl], rhs=B[0:64, sl])
                A2, B2 = m64(), m64()
                P.copy(v4(A2), psA2[0:64, 0:256], eng="act")
                P.copy(v4(B2), psB2[0:64, 0:256], eng="dve")
                psY = P.ps()
                for h in range(4):
                    sl = slice(h * 64, (h + 1) * 64)
                    P.mm(psY[0:64, sl], lhsT=A2[0:64, sl], rhs=Y[0:64, sl])
                Y2 = m64()
                P.tt(v4(Y2), v4(Y), psY[0:64, 0:256], ALU.add)
                A, B, Y = A2, B2, Y2
            XTu, XTw = m64(), m64()
            P.tt(h3(v4(XTu)), h3(v4(Y)), bc4(be4), ALU.mult)
            P.tt(h3(v4(XTw)), h3(v4(Y)), bc4(bw), ALU.mult)
            psU, psW = P.ps(), P.ps()
            for h in range(4):
                P.mm(psU[0:64, h * 128:(h + 1) * 128], lhsT=XTu[0:64, h * 64:(h + 1) * 64], rhs=vt[0:64, h * 128:(h + 1) * 128])
            for h in range(4):
                P.mm(psW[:, h * 64:(h + 1) * 64], lhsT=kt[0:64, h * 128:(h + 1) * 128], rhs=XTw[0:64, h * 64:(h + 1) * 64])
            U, WT, KD_ = m512(), m64(), m512()
            P.copy(U[0:64, :], psU[0:64, :], eng="act")
            P.copy(WT[:, 0:256], psW[:, 0:256], eng="dve")
            P.tt(KD_[0:64, :].rearrange("p (h n) -> p h n", h=4), kt[0:64, :].rearrange("p (h n) -> p h n", h=4),
                 bc4(e2, 64, 128), ALU.mult)
            Sd = S[d]
            psWS = P.ps()
            for h in range(4):
                P.mm(psWS[0:64, h * 128:(h + 1) * 128], lhsT=WT[:, h * 64:(h + 1) * 64], rhs=Sd[:, h * 128:(h + 1) * 128])
            VN = m512()
            P.tt(VN[0:64, :], U[0:64, :], psWS[0:64, :], ALU.subtract)
            psO1, psO2 = P.ps(), P.ps()
            for h in range(4):
                P.mm(psO1[0:64, h * 128:(h + 1) * 128], lhsT=qf[:, h * 64:(h + 1) * 64], rhs=Sd[:, h * 128:(h + 1) * 128])
            for h in range(4):
                P.mm(psO2[0:64, h * 128:(h + 1) * 128], lhsT=AT[0:64, h * 64:(h + 1) * 64], rhs=VN[0:64, h * 128:(h + 1) * 128])
            O = m512()
            P.tt(O[0:64, :].rearrange("p (h n) -> p h n", h=4), psO1[0:64, :].rearrange("p (h n) -> p h n", h=4),
                 bc4(eg, 64, 128), ALU.mult)
            P.tt(O[0:64, :], O[0:64, :], psO2[0:64, :], ALU.add)
            P.dma("pool", OD[d].ap[cs0:cs0 + 64, :], O[0:64, :], w=OD[d].tk(cs0, 64))
            psS = P.ps()
            for h in range(4):
                P.mm(psS[:, h * 128:(h + 1) * 128], lhsT=KD_[0:64, h * 128:(h + 1) * 128], rhs=VN[0:64, h * 128:(h + 1) * 128])
            for h in range(4):
                sl = slice(h * 128, (h + 1) * 128)
                P.stt(Sd[:, sl], Sd[:, sl], egl[:, h:h + 1], psS[:, sl], ALU.mult, ALU.add)

        nchk = dbg.get("nchunks", TALL // 64)
        li = 0
        for step in range(nchk):
            for d, order in ((0, fwd_order), (1, bwd_order)):
                ci = order[step]
                slot = li % NB
                li += 1
                load_chunk(ci, slot)
                chunk(ci, d, slot)
        if stop == "p4b":
            return True

        P.arena_reset()
        o0s = [P.av(512) for _ in range(2)]
        o1s = [P.av(512) for _ in range(2)]
        ogs = [P.av(512) for _ in range(2)]
        wk = Rot([P.av(512) for _ in range(4)])
        s4 = Rot([P.av(8) for _ in range(2)])
        wkh = Rot([P.av(512, BF16) for _ in range(2)])
        tl = range(0, TALL, 128) if not last else range(NCTX, TALL, 128)
        for ti, t0 in enumerate(tl):
            o0, o1, og = o0s[ti % 2], o1s[ti % 2], ogs[ti % 2]
            P.dma("sp", o0[:, :], OD[0].ap[t0:t0 + 128, :], r=OD[0].tk(t0, 128))
            P.dma("sp", o1[:, :], OD[1].ap[t0:t0 + 128, :], r=OD[1].tk(t0, 128))
            P.dma("sp", og[:, :], OGS.ap[t0:t0 + 128, :], r=OGS.tk(t0, 128))
            P.tt(o0[:, :], o0[:, :], o1[:, :], ALU.add)
            sq = wk()
            P.act(sq[:, :], o0[:, :], AF.Square)
            ss = s4()
            P.reduce(ss[:, 0:4], sq[:, :].rearrange("p (h n) -> p h n", h=4), ALU.add)
            P.act(ss[:, 0:4], ss[:, 0:4], AF.Sqrt, scale=1.0 / 128, bias=EPS)
            P.recip(ss[:, 0:4], ss[:, 0:4])
            P.tt(o0[:, :].rearrange("p (h n) -> p h n", h=4), o0[:, :].rearrange("p (h n) -> p h n", h=4),
                 ss[:, 0:4].unsqueeze(2).to_broadcast([128, 4, 128]), ALU.mult)
            P.tt(o0[:, :], o0[:, :], vb_("onorm"), ALU.mult)
            P.tt(o0[:, :], o0[:, :], og[:, :], ALU.mult)
            ps = P.ps()
            for c in range(4):
                P.tr(ps[:, c * 128:(c + 1) * 128], o0[:, c * 128:(c + 1) * 128], ident[:, :])
            yT = wkh()
            P.copy(yT[:, :], ps[:, :], eng="act")
            P.dma("pool", YC.ap[:, t0:t0 + 128].rearrange("(c p) t -> p c t", p=128),
                  yT[:, :].rearrange("p (c t) -> p c t", c=4), w=YC.tk(t0, 128))
        if stop == "p4d":
            return True

        P.arena_reset()
        wbr = P.av(12 * D, BF16)
        wo = P.av(8 * D, BF16)
        P.dma("pool", wbr[:, :].rearrange("p (k n) -> p k n", k=12), w_br[l])
        P.dma("pool", wo[:, :].rearrange("p (k n) -> p k n", k=8), w_out[l])
        YTs = [P.av(12 * 512, BF16) for _ in range(2)]
        MGs = [P.av(24 * 512, BF16) for _ in range(2)]
        ZT = P.av(8 * 512, BF16)
        XT = [P.av(D) for _ in range(2)]
        XO = [P.av(D) for _ in range(2)]
        wk = Rot([P.av(512) for _ in range(4)])
        tiles5 = LIN_TILES if not last else LIN_TILES[1:]
        xi = 0
        for ti, (t0, nt) in enumerate(tiles5):
            yt, mg = YTs[ti % 2], MGs[ti % 2]
            for bi, src in enumerate((YA, YB, YC)):
                P.dma("sp", yt[:, bi * 4 * nt:(bi + 1) * 4 * nt].rearrange("p (c t) -> p c t", c=4),
                      src.ap[:, t0:t0 + nt].rearrange("(c p) t -> p c t", p=128), r=src.tk(t0, nt))
            P.dma("sp", mg[:, 0:24 * nt].rearrange("p (c t) -> p c t", c=24),
                  MG.ap[:, t0:t0 + nt].rearrange("(c p) t -> p c t", p=128), r=MG.tk(t0, nt))
            for n_ in range(8):
                z = wk()
                for br in range(3):
                    ps = P.ps()
                    for kc in range(4):
                        k = br * 4 + kc
                        P.mm(ps[:, :nt], lhsT=wbr[:, k * D + n_ * 128:k * D + (n_ + 1) * 128],
                             rhs=yt[:, k * nt:(k + 1) * nt], start=(kc == 0), stop=(kc == 3))
                    gv = mg[:, (br * 8 + n_) * nt:(br * 8 + n_ + 1) * nt]
                    if br == 0:
                        P.tt(z[:, :nt], ps[:, :nt], gv, ALU.mult)
                    else:
                        tmp = wk()
                        P.tt(tmp[:, :nt], ps[:, :nt], gv, ALU.mult)
                        if br == 1:
                            P.tt(z[:, :nt], z[:, :nt], tmp[:, :nt], ALU.add, eng="pool")
                        else:
                            P.tt(ZT[:, n_ * nt:(n_ + 1) * nt], z[:, :nt], tmp[:, :nt], ALU.add, eng="pool")
            for s0 in range(0, nt, 128):
                tg = t0 + s0
                ap, tks, j = src_of(tg, x_src, c_src)
                xt, xo = XT[xi % 2], XO[xi % 2]
                xi += 1
                P.dma("sp", xt[:, :], ap, r=tks)
                for half in range(2):
                    ps = P.ps()
                    for kc in range(8):
                        P.mm(ps[:, :], lhsT=ZT[:, kc * nt + s0:kc * nt + s0 + 128],
                             rhs=wo[:, kc * D + half * 512:kc * D + (half + 1) * 512], start=(kc == 0), stop=(kc == 7))
                    hs = slice(half * 512, (half + 1) * 512)
                    P.tt(xo[:, hs], ps[:, :], gbc[0][j][:, hs], ALU.mult)
                    P.tt(xo[:, hs], xo[:, hs], xt[:, hs], ALU.add, eng="pool")
                if tg < NCTX:
                    P.dma("pool", c1_dst.ap[tg:tg + 128, :], xo[:, :], w=c1_dst.tk(tg, 128))
                else:
                    P.dma("pool", x1_dst.ap[tg - NCTX:tg - NCTX + 128, :], xo[:, :], w=x1_dst.tk(tg - NCTX, 128))
        if stop == "p5":
            return True

        P.arena_reset()
        hT = P.av(8 * TALL, BF16)
        WB = [P.av(8 * 256, BF16) for _ in range(2)]
        XT = [P.av(D) for _ in range(2)]
        XN = [P.av(D) for _ in range(2)]
        ST = [P.av(8) for _ in range(2)]
        ubs = Rot([P.av(516) for _ in range(4)])
        wk = Rot([P.av(512) for _ in range(6)])
        wkh = Rot([P.av(512, BF16) for _ in range(3)])
        tstart = NCTX if last else 0
        for ti, t0 in enumerate(range(tstart, TALL, 128)):
            ap, tks, j = src_of(t0, x1_dst, c1_dst)
            norm_tile(ap, tks, lambda c, t0=t0: hT[:, c * TALL + t0:c * TALL + t0 + 128], 1, j,
                      XT[ti % 2], XN[ti % 2], ST[ti % 2])
        ftiles = []
        for (s0, sn) in (SEGS if not last else SEGS[1:]):
            for o0 in range(s0, s0 + sn, 510):
                no = min(510, s0 + sn - o0)
                ftiles.append((o0, no, max(o0 - 1, s0), min(o0 + no + 1, s0 + sn)))

        def load_up(i):
            wb = WB[i % 2]
            wv = wb[:, 0:8 * 256].rearrange("p (k n) -> p k n", k=8)
            P.dma("pool", wv[:, :, 0:128], w_up[l][:, :, i * 128:(i + 1) * 128])
            P.dma("pool", wv[:, :, 128:256], w_up[l][:, :, DFF + i * 128:DFF + (i + 1) * 128])
        load_up(0)
        o_fw, _ = VFM["fcw"]
        for i in range(22):
            if i + 1 < 22:
                load_up(i + 1)
            wb = WB[i % 2]
            for (o0, no, a, b) in ftiles:
                ncol = b - a
                convs = []
                for part in range(2):
                    chn = i if part == 0 else 22 + i
                    ps = P.ps()
                    for kc in range(8):
                        P.mm(ps[:, 0:ncol], lhsT=wb[:, kc * 256 + part * 128:kc * 256 + (part + 1) * 128],
                             rhs=hT[:, kc * TALL + a:kc * TALL + b], start=(kc == 0), stop=(kc == 7))
                    ub = ubs()
                    off = a - (o0 - 1)
                    if off > 0:
                        P.memset(ub[:, 0:1], 0.0)
                    if b < o0 + no + 1:
                        P.memset(ub[:, off + ncol:off + ncol + 1], 0.0)
                    P.copy(ub[:, off:off + ncol], ps[:, 0:ncol], eng="act")
                    acc = wk()
                    eng = "dve" if part == 0 else "pool"
                    wc = o_fw + chn * 3
                    P.ts(acc[:, :no], ub[:, 0:no], vfm[:, wc:wc + 1], vf("fcb", chn), ALU.mult, ALU.add, eng=eng)
                    P.stt(acc[:, :no], ub[:, 1:1 + no], vfm[:, wc + 1:wc + 2], acc[:, :no], ALU.mult, ALU.add, eng=eng)
                    P.stt(acc[:, :no], ub[:, 2:2 + no], vfm[:, wc + 2:wc + 3], acc[:, :no], ALU.mult, ALU.add, eng=eng)
                    convs.append(acc)
                sg = wk()
                P.act(sg[:, :no], convs[0][:, :no], AF.Silu)
                ah = wkh()
                P.tt(ah[:, :no], sg[:, :no], convs[1][:, :no], ALU.mult)
                P.dma("pool", ACTT.ap[i * 128:(i + 1) * 128, o0:o0 + no], ah[:, :no], w=ACTT.tk(o0, no))
        if stop == "p6a":
            return True

        P.arena_reset()
        wd = P.av(22 * D, BF16)
        P.dma("pool", wd[:, :].rearrange("p (k n) -> p k n", k=22), w_dn[l])
        ATs = [P.av(22 * 512, BF16) for _ in range(2)]
        XT = [P.av(D) for _ in range(2)]
        XO = [P.av(D) for _ in range(2)]
        xi = 0
        for ti, (t0, nt) in enumerate(LIN_TILES if not last else LIN_TILES[1:]):
            at = ATs[ti % 2]
            P.dma("sp", at[:, 0:22 * nt].rearrange("p (c t) -> p c t", c=22),
                  ACTT.ap[:, t0:t0 + nt].rearrange("(c p) t -> p c t", p=128), r=ACTT.tk(t0, nt))
            for s0 in range(0, nt, 128):
                tg = t0 + s0
                ap, tks, j = src_of(tg, x1_dst, c1_dst)
                xt, xo = XT[xi % 2], XO[xi % 2]
                xi += 1
                P.dma("sp", xt[:, :], ap, r=tks)
                for half in range(2):
                    ps = P.ps()
                    for kc in range(22):
                        P.mm(ps[:, :], lhsT=at[:, kc * nt + s0:kc * nt + s0 + 128],
                             rhs=wd[:, kc * D + half * 512:kc * D + (half + 1) * 512], start=(kc == 0), stop=(kc == 21))
                    hs = slice(half * 512, (half + 1) * 512)
                    P.tt(xo[:, hs], ps[:, :], gbc[1][j][:, hs], ALU.mult)
                    P.tt(xo[:, hs], xo[:, hs], xt[:, hs], ALU.add, eng="pool")
                if tg < NCTX:
                    P.dma("pool", c2_dst.ap[tg:tg + 128, :], xo[:, :], w=c2_dst.tk(tg, 128))
                else:
                    P.dma("pool", x2_dst.ap[tg - NCTX:tg - NCTX + 128, :], xo[:, :], w=x2_dst.tk(tg - NCTX, 128))
        return False

    x0 = DT(x_in, 0, NLAT)
    c0 = DT(ctx_in, 0, NCTX)
    yo = DT(y_out, 0, NLAT)
    stopped = layer(0, x0, c0, X1, C1, X2, C2, NL == 1 and dbg.get("last0", False))
    if not stopped and NL > 1:
        layer(1, X2, C2, X1, C1, yo, None, True)
    P.finish()
    P.emit()
    print("instructions:", P.nins)
    return nc, es


def host_layouts(inp, b):
    f = np.float32
    m = {}
    m["x"] = np.ascontiguousarray(inp["x"][b])
    m["ctx"] = np.ascontiguousarray(inp["ctx"][b])
    cv = np.zeros((128, 8, 2), f)
    cv[:, :, 0] = inp["c"][b].reshape(8, 128).T
    cv[:, :, 1] = inp["c_ctx"].reshape(8, 128).T
    m["cvec"] = cv.reshape(128, 16)
    return m


def na_index_tables():
    rows = 64

    def band(r):
        s = min(max(r - 4, 0), rows - 8)
        return range(s, s + 8)

    cidx = np.arange(64)
    cs = np.clip(cidx - 8, 0, 64 - 16)
    col_ok = (cidx[None, :] >= cs[:, None]) & (cidx[None, :] < cs[:, None] + 16)
    dcq = np.clip(cidx[None, :] - cidx[:, None] + 15, 0, 30)

    def gen(r0, rk0):
        dr = np.zeros((128, 128), np.int64)
        dc = np.zeros((128, 128), np.int64)
        ok = np.zeros((128, 128), bool)
        for a in range(2):
            for b in range(2):
                kr, qr = rk0 + a, r0 + b
                if kr in band(qr):
                    dr[a * 64:(a + 1) * 64, b * 64:(b + 1) * 64] = kr - qr + 7
                    dc[a * 64:(a + 1) * 64, b * 64:(b + 1) * 64] = dcq.T
                    ok[a * 64:(a + 1) * 64, b * 64:(b + 1) * 64] = col_ok.T
        return dr, dc, ok

    reps = [(62, 56), (62, 58), (62, 60), (62, 62), (0, 2), (0, 4), (0, 6),
            (30, 26), (30, 28), (30, 30), (30, 32), (30, 34)]
    tabs = [gen(*r) for r in reps]
    plan = na_plan()
    for qi, (kr0, nk, case0) in enumerate(plan):
        for j in range(nk):
            dr, dc, ok = gen(qi * 2, kr0 + 2 * j)
            tdr, tdc, tok = tabs[case0 + j]
            assert np.array_equal(ok, tok) and np.array_equal(dr[ok], tdr[ok]) and np.array_equal(dc[ok], tdc[ok]), (qi, j)
    return tabs


def shared_layouts(inp, nl=DEPTH):
    f = np.float32
    s = {}

    def kmaj(w, kc):
        L, K, N = w.shape
        return np.ascontiguousarray(w.reshape(L, kc, 128, N).transpose(0, 2, 1, 3))

    s["ada_w"] = kmaj(inp["ada_w"][:nl], 8)
    s["w_in"] = kmaj(inp["w_in"][:nl], 8)
    s["w_br"] = kmaj(inp["w_branch"][:nl].reshape(nl, 1536, D), 12)
    s["w_out"] = kmaj(inp["w_out"][:nl], 8)
    s["w_up"] = kmaj(inp["ffn_up"][:nl], 8)
    s["w_dn"] = kmaj(inp["ffn_down"][:nl], 22)
    vfm = np.zeros((nl, 128, NVFM), f)

    def put(name, arr):
        o, n = VFM[name]
        vfm[:, :, o:o + n] = np.asarray(arr)[:nl].reshape(nl, 128, n)

    def fmv(v, nch):
        return v.reshape(DEPTH, nch, 128).transpose(0, 2, 1)

    put("g1", fmv(inp["norm1_g"], 8))
    put("g2", fmv(inp["norm2_g"], 8))
    put("adab", fmv(inp["ada_b"], 48))
    put("caw", inp["conv_a_w"].reshape(DEPTH, 31, 4, 128).transpose(0, 3, 2, 1))
    put("cab", fmv(inp["conv_a_b"], 4))
    put("lag", fmv(inp["ln_a_g"], 4))
    put("lab", fmv(inp["ln_a_b"], 4))
    put("qng", np.tile(inp["qn_g"], (1, 2)).reshape(DEPTH, 128, 1))
    put("kng", np.tile(inp["kn_g"], (1, 2)).reshape(DEPTH, 128, 1))
    put("ccw", inp["conv_c_w"].reshape(DEPTH, 4, 12, 128).transpose(0, 3, 2, 1))
    put("fcw", inp["ffn_conv_w"].reshape(DEPTH, 3, 44, 128).transpose(0, 3, 2, 1))
    put("fcb", fmv(inp["ffn_conv_b"], 44))
    s["vfm"] = vfm
    vbc = np.zeros((nl, 128, NVBC), f)

    def putb(name, row):
        o, n = VBC[name]
        vbc[:, :, o:o + n] = np.asarray(row)[:nl].reshape(nl, 1, n)

    putb("onorm", np.tile(inp["onorm_g"], (1, 4)))
    putb("alog", inp["a_log"].reshape(DEPTH, 8))
    putb("dtb", inp["dt_bias"].reshape(DEPTH, 8))
    s["vbc"] = vbc
    tabs = na_index_tables()
    nab = np.full((nl, 128, NCASE, 8, 128), NEG, f)
    for ci, (dr, dc, ok) in enumerate(tabs):
        for l in range(nl):
            g = inp["rpb"][l][:, dr, dc]
            g = np.where(ok[None], g, f(NEG))
            nab[l, :, ci, :, :] = g.transpose(1, 0, 2)
    s["nab"] = nab.reshape(nl, 128, NCASE * 8 * 128)
    s["ident"] = np.eye(128, dtype=f)
    rm = np.zeros((128, 128), f)
    for d in range(128):
        if d % 64 < 32:
            rm[d + 32, d] = -1.0
        else:
            rm[d - 32, d] = 1.0
    s["rm"] = rm
    g = np.zeros((128, NGCON), f)

    def putg(name, arr):
        o, n = GCON[name]
        g[0:64, o:o + n] = arr

    p = np.arange(64)[:, None]
    q = np.arange(64)[None, :]
    tri0 = (p <= q).astype(f)
    tri1 = (p >= q).astype(f)
    putg("tri0", tri0)
    putg("tri1", tri1)
    putg("ntri0", -tri0)
    putg("ntri1", -tri1)
    putg("bigd0", np.tile(np.where(p < q, 30000.0, 0.0).astype(f), (1, 4)))
    putg("bigd1", np.tile(np.where(p > q, 30000.0, 0.0).astype(f), (1, 4)))
    putg("nstr0", np.tile(np.where(p > q, -1.0, 0.0).astype(f), (1, 4)))
    putg("nstr1", np.tile(np.where(p < q, -1.0, 0.0).astype(f), (1, 4)))
    putg("iden4", np.tile(np.eye(64, dtype=f), (1, 4)))
    s["gcon"] = g
    t = np.arange(NLAT)
    row = (t // GRID_W).astype(f)
    col = (t % GRID_W).astype(f)
    inv = np.power(f(10000.0), -np.arange(32, dtype=f) / f(32)).astype(f)
    ang = np.zeros((128, NLAT), f)
    for d in range(128):
        ang[d] = (row if d < 64 else col) * inv[d % 32]
    s["ropec"] = np.cos(ang).astype(f)
    s["ropes"] = np.sin(ang).astype(f)
    return s


def kernel(**inp):
    inp = {k: np.asarray(v) for k, v in inp.items()}
    nc, es = build()
    shared = shared_layouts(inp)
    in_maps = []
    for core in range(8):
        if core < 4:
            m = dict(shared)
            m.update(host_layouts(inp, core))
        else:
            m = {k: np.zeros_like(v) for k, v in in_maps[0].items()}
        in_maps.append(m)
    res = run_bass_kernel_spmd(nc, in_maps, core_ids=list(range(8)))
    es.close()
    return np.stack([np.asarray(res.results[b]["y"]) for b in range(4)], 0)
```
